# Optimizing a Trainium2 kernel written in Bass

```python
import jax, jax.numpy as jnp
from jax import lax
import numpy as np

D_MODEL = 1024
BATCH = 8
SEQ = 4096
DEPTH = 2
DEC_BATCH = 1
DEC_SEQ = 16384
PAST_LEN = 128

N_MEM = 256
D_FF = 2816
CONV_DIM = 512
CONV_WIDTH = 31
RET_HEADS = 8
RET_HEAD_DIM = 64
RET_DIM = RET_HEADS * RET_HEAD_DIM
RET_CHUNK = 128
XATTN_HEADS = 4
XATTN_HEAD_DIM = D_MODEL // XATTN_HEADS
ROPE_BASE = 10000.0
NORM_EPS = 1e-6
N_NORMS = 8
IN_COLS = 2 * CONV_DIM + 4 * RET_DIM + 2 * D_MODEL

kernel_name = 'hybrid_conv_retention_encoder'


def rmsnorm(x, g):
    xf = x.astype(jnp.float32)
    y = xf * lax.rsqrt(jnp.mean(xf * xf, axis=-1, keepdims=True) + NORM_EPS)
    return (y * g.astype(jnp.float32)).astype(x.dtype)


def swiglu_ffn(h, w_gu, w_down):
    gate, up = jnp.split(h @ w_gu, 2, axis=-1)
    return (jax.nn.silu(gate) * up) @ w_down


def rotary(x):
    s, dh = x.shape[1], x.shape[-1]
    half = dh // 2
    inv_freq = ROPE_BASE ** (-jnp.arange(half, dtype=jnp.float32) / half)
    ang = jnp.arange(s, dtype=jnp.float32)[:, None] * inv_freq[None, :]
    cos = jnp.cos(ang)[None, :, None, :]
    sin = jnp.sin(ang)[None, :, None, :]
    xf = x.astype(jnp.float32)
    x1, x2 = xf[..., :half], xf[..., half:]
    return jnp.concatenate([x1 * cos - x2 * sin, x1 * sin + x2 * cos], axis=-1)


def retention_direction(q, k, v, decay_param, include_diag):
    b, s, h, dh = q.shape
    c = RET_CHUNK
    n = s // c
    qc = q.reshape(b, n, c, h, dh)
    kc = k.reshape(b, n, c, h, dh)
    vc = v.reshape(b, n, c, h, dh)
    log_gamma = -jnp.exp(decay_param.astype(jnp.float32))
    pos = jnp.arange(c, dtype=jnp.float32)
    rel = pos[:, None] - pos[None, :]
    mask = (rel >= 0) if include_diag else (rel > 0)
    intra_decay = jnp.where(mask[None], jnp.exp(log_gamma[:, None, None] * jnp.where(mask, rel, 0.0)[None]), 0.0)
    scores = jnp.einsum('bnihd,bnjhd->bnhij', qc, kc) * intra_decay[None, None]
    intra = jnp.einsum('bnhij,bnjhd->bnihd', scores, vc)
    k_dec = jnp.exp(log_gamma[None, :] * (c - 1.0 - pos)[:, None])
    chunk_kv = jnp.einsum('bnjhd,bnjhe->nbhde', kc * k_dec[None, None, :, :, None], vc)
    chunk_decay = jnp.exp(log_gamma * c)[None, :, None, None]

    def step(state, kv):
        return state * chunk_decay + kv, state

    _, prev = lax.scan(step, jnp.zeros((b, h, dh, dh), jnp.float32), chunk_kv)
    q_dec = jnp.exp(log_gamma[None, :] * (pos + 1.0)[:, None])
    cross = jnp.einsum('bnihd,nbhde->bnihe', qc * q_dec[None, None, :, :, None], prev)
    return (intra + cross).reshape(b, s, h, dh)


def retention_branch(q, k, v, g, decay_fwd, decay_bwd, gn_g, w_out):
    b, s, _ = q.shape
    shp = (b, s, RET_HEADS, RET_HEAD_DIM)
    qr = rotary(q.reshape(shp))
    kr = rotary(k.reshape(shp)) * (RET_HEAD_DIM ** -0.5)
    vf = v.reshape(shp).astype(jnp.float32)
    fwd = retention_direction(qr, kr, vf, decay_fwd, True)
    bwd = jnp.flip(retention_direction(jnp.flip(qr, 1), jnp.flip(kr, 1), jnp.flip(vf, 1), decay_bwd, False), 1)
    y = fwd + bwd
    mu = jnp.mean(y, axis=-1, keepdims=True)
    var = jnp.mean(jnp.square(y - mu), axis=-1, keepdims=True)
    y = (y - mu) * lax.rsqrt(var + NORM_EPS) * gn_g.astype(jnp.float32).reshape(RET_HEADS, RET_HEAD_DIM)
    y = y.reshape(b, s, RET_DIM).astype(q.dtype)
    return (jax.nn.silu(g) * y) @ w_out


def conformer_conv_branch(a, dw_w, dw_b, ln_g, ln_b, w_pw):
    val, gate = jnp.split(a, 2, axis=-1)
    u = val * jax.nn.sigmoid(gate)
    pad = CONV_WIDTH // 2
    u = lax.conv_general_dilated(u, dw_w[:, None, :].astype(u.dtype), window_strides=(1,), padding=[(pad, pad)],
                                 dimension_numbers=('NWC', 'WIO', 'NWC'), feature_group_count=CONV_DIM) + dw_b
    uf = u.astype(jnp.float32)
    mu = jnp.mean(uf, axis=-1, keepdims=True)
    var = jnp.mean(jnp.square(uf - mu), axis=-1, keepdims=True)
    uf = (uf - mu) * lax.rsqrt(var + NORM_EPS) * ln_g.astype(jnp.float32) + ln_b.astype(jnp.float32)
    return jax.nn.silu(uf).astype(a.dtype) @ w_pw


def memory_cross_attention(h, mem, mem_g, w_q, w_kv, w_o):
    b, s, _ = h.shape
    m = mem.shape[1]
    q = (h @ w_q).reshape(b, s, XATTN_HEADS, XATTN_HEAD_DIM)
    kv = (rmsnorm(mem, mem_g) @ w_kv).reshape(b, m, 2, XATTN_HEADS, XATTN_HEAD_DIM)
    k, v = kv[:, :, 0], kv[:, :, 1]
    logits = jnp.einsum('bshd,bmhd->bhsm', q.astype(jnp.float32), k.astype(jnp.float32)) * (XATTN_HEAD_DIM ** -0.5)
    p = jax.nn.softmax(logits, axis=-1).astype(v.dtype)
    o = jnp.einsum('bhsm,bmhd->bshd', p, v).reshape(b, s, D_MODEL)
    return o @ w_o


def trunk(x, mem, norm_g, ffn1_w_gu, ffn1_w_down, w_in, conv_dw_w, conv_dw_b, conv_ln_g, conv_ln_b, conv_w_pw,
          ret_decay_fwd, ret_decay_bwd, ret_gn_g, ret_w_out, gate_b, w_mix_out, mem_norm_g,
          xattn_w_q, xattn_w_kv, xattn_w_o, ffn2_w_gu, ffn2_w_down):
    splits = [2 * CONV_DIM, 2 * CONV_DIM + RET_DIM, 2 * CONV_DIM + 2 * RET_DIM,
              2 * CONV_DIM + 3 * RET_DIM, 2 * CONV_DIM + 4 * RET_DIM]
    for l in range(DEPTH):
        f = swiglu_ffn(rmsnorm(x, norm_g[l, 0]), ffn1_w_gu[l], ffn1_w_down[l])
        x = x + 0.5 * rmsnorm(f, norm_g[l, 1])
        h = rmsnorm(x, norm_g[l, 2])
        proj = h @ w_in[l]
        conv_in, q, k, v, g, gate_logits = jnp.split(proj, splits, axis=-1)
        conv_out = conformer_conv_branch(conv_in, conv_dw_w[l], conv_dw_b[l], conv_ln_g[l], conv_ln_b[l], conv_w_pw[l])
        ret_out = retention_branch(q, k, v, g, ret_decay_fwd[l], ret_decay_bwd[l], ret_gn_g[l], ret_w_out[l])
        gate_conv, gate_ret = jnp.split(jax.nn.sigmoid(gate_logits + gate_b[l]), 2, axis=-1)
        mixed = (gate_conv * conv_out + gate_ret * ret_out) @ w_mix_out[l]
        x = x + rmsnorm(mixed, norm_g[l, 3])
        a = memory_cross_attention(rmsnorm(x, norm_g[l, 4]), mem, mem_norm_g[l], xattn_w_q[l], xattn_w_kv[l], xattn_w_o[l])
        x = x + rmsnorm(a, norm_g[l, 5])
        f = swiglu_ffn(rmsnorm(x, norm_g[l, 6]), ffn2_w_gu[l], ffn2_w_down[l])
        x = x + 0.5 * rmsnorm(f, norm_g[l, 7])
    return x


def setup_inputs(seed: int = 0) -> dict:
    key = jax.random.key(seed)
    ks = jax.random.split(key, 32)

    def w(k, shape, fan_in):
        return jax.random.normal(k, shape, jnp.float32) * (fan_in ** -0.5)

    def gain(k, shape):
        return 1.0 + 0.05 * jax.random.normal(k, shape, jnp.float32)

    def small(k, shape):
        return 0.02 * jax.random.normal(k, shape, jnp.float32)

    decay_base = jnp.log(-jnp.log1p(-(2.0 ** (-5.0 - jnp.arange(RET_HEADS, dtype=jnp.float32)))))
    return {
        'x_prompt': jax.random.normal(ks[0], (BATCH, SEQ, D_MODEL), jnp.float32),
        'x_sample': jax.random.normal(ks[1], (DEC_BATCH, DEC_SEQ, D_MODEL), jnp.float32),
        'mem_prompt': jax.random.normal(ks[2], (BATCH, N_MEM, D_MODEL), jnp.float32),
        'mem_sample': jax.random.normal(ks[3], (DEC_BATCH, N_MEM, D_MODEL), jnp.float32),
        'norm_g': gain(ks[4], (DEPTH, N_NORMS, D_MODEL)),
        'ffn1_w_gu': w(ks[5], (DEPTH, D_MODEL, 2 * D_FF), D_MODEL),
        'ffn1_w_down': w(ks[6], (DEPTH, D_FF, D_MODEL), D_FF),
        'w_in': w(ks[7], (DEPTH, D_MODEL, IN_COLS), D_MODEL),
        'conv_dw_w': w(ks[8], (DEPTH, CONV_WIDTH, CONV_DIM), CONV_WIDTH),
        'conv_dw_b': small(ks[9], (DEPTH, CONV_DIM)),
        'conv_ln_g': gain(ks[10], (DEPTH, CONV_DIM)),
        'conv_ln_b': small(ks[11], (DEPTH, CONV_DIM)),
        'conv_w_pw': w(ks[12], (DEPTH, CONV_DIM, D_MODEL), CONV_DIM),
        'ret_decay_fwd': decay_base[None, :] + 0.1 * jax.random.normal(ks[13], (DEPTH, RET_HEADS), jnp.float32),
        'ret_decay_bwd': decay_base[None, :] + 0.1 * jax.random.normal(ks[14], (DEPTH, RET_HEADS), jnp.float32),
        'ret_gn_g': gain(ks[15], (DEPTH, RET_DIM)),
        'ret_w_out': w(ks[16], (DEPTH, RET_DIM, D_MODEL), RET_DIM),
        'gate_b': small(ks[17], (DEPTH, 2 * D_MODEL)),
        'w_mix_out': w(ks[18], (DEPTH, D_MODEL, D_MODEL), D_MODEL),
        'mem_norm_g': gain(ks[19], (DEPTH, D_MODEL)),
        'xattn_w_q': w(ks[20], (DEPTH, D_MODEL, D_MODEL), D_MODEL),
        'xattn_w_kv': w(ks[21], (DEPTH, D_MODEL, 2 * D_MODEL), D_MODEL),
        'xattn_w_o': w(ks[22], (DEPTH, D_MODEL, D_MODEL), D_MODEL),
        'ffn2_w_gu': w(ks[23], (DEPTH, D_MODEL, 2 * D_FF), D_MODEL),
        'ffn2_w_down': w(ks[24], (DEPTH, D_FF, D_MODEL), D_FF),
    }


def reference(x_prompt, x_sample, mem_prompt, mem_sample, norm_g, ffn1_w_gu, ffn1_w_down, w_in,
              conv_dw_w, conv_dw_b, conv_ln_g, conv_ln_b, conv_w_pw, ret_decay_fwd, ret_decay_bwd,
              ret_gn_g, ret_w_out, gate_b, w_mix_out, mem_norm_g, xattn_w_q, xattn_w_kv, xattn_w_o,
              ffn2_w_gu, ffn2_w_down):
    y_prompt = trunk(x_prompt, mem_prompt, norm_g, ffn1_w_gu, ffn1_w_down, w_in, conv_dw_w, conv_dw_b,
                     conv_ln_g, conv_ln_b, conv_w_pw, ret_decay_fwd, ret_decay_bwd, ret_gn_g, ret_w_out,
                     gate_b, w_mix_out, mem_norm_g, xattn_w_q, xattn_w_kv, xattn_w_o, ffn2_w_gu, ffn2_w_down)
    y_sample = trunk(x_sample, mem_sample, norm_g, ffn1_w_gu, ffn1_w_down, w_in, conv_dw_w, conv_dw_b,
                     conv_ln_g, conv_ln_b, conv_w_pw, ret_decay_fwd, ret_decay_bwd, ret_gn_g, ret_w_out,
                     gate_b, w_mix_out, mem_norm_g, xattn_w_q, xattn_w_kv, xattn_w_o, ffn2_w_gu, ffn2_w_down)
    return (y_prompt, y_sample)
```

```python
import contextlib
import numpy as np
import concourse.bass as bass
import concourse.mybir as mybir
from concourse.bass_utils import run_bass_kernel_spmd

F32 = mybir.dt.float32
BF16 = mybir.dt.bfloat16
ALU = mybir.AluOpType
AF = mybir.ActivationFunctionType
AX = mybir.AxisListType

D = 1024
DFF = 2816
NJ = 22
CONV = 512
RET = 512
H = 8
DH = 64
NMEM = 256
INC = 5120
CW = 31
EPS = 1e-6
NCORE = 8


class Eng:
    def __init__(self, name, eng, sem):
        self.name, self.eng, self.sem, self.cnt, self.seen = name, eng, sem, 0, {}


class Src:
    def __init__(self, sem):
        self.sem, self.cnt, self.name = sem, 0, "dma"


class T:
    __slots__ = ("w", "r", "ps")

    def __init__(self, ps=False):
        self.w = None
        self.r = {}
        self.ps = ps


class FW:
    def __init__(self, nc, es, n_dma_sems=20):
        self.nc = nc
        mk = lambda n: es.enter_context(nc.semaphore(n))
        self.pe = Eng("pe", nc.tensor, mk("s_pe"))
        self.act = Eng("act", nc.scalar, mk("s_act"))
        self.dve = Eng("dve", nc.vector, mk("s_dve"))
        self.pool = Eng("pool", nc.gpsimd, mk("s_pool"))
        self.sp = Eng("sp", nc.sync, mk("s_sp"))
        self.engs = [self.pe, self.act, self.dve, self.pool, self.sp]
        self.dsems = {"sp": [Src(mk(f"d_sp{i}")) for i in range(n_dma_sems)],
                      "pool": [Src(mk(f"d_pl{i}")) for i in range(n_dma_sems)],
                      "act": [Src(mk(f"d_ac{i}")) for i in range(8)]}
        self.dnext = {"sp": 0, "pool": 0, "act": 0}
        self.ccsem = Src(mk("s_cc"))

    def _waits(self, e, R, W):
        deps = {}
        for t in R:
            if t.w is not None and deps.get(t.w[0], 0) < t.w[1]:
                deps[t.w[0]] = t.w[1]
            if t.ps:
                for s, c in t.r.items():
                    if s is not e and deps.get(s, 0) < c:
                        deps[s] = c
        for t in W:
            if t.w is not None and deps.get(t.w[0], 0) < t.w[1]:
                deps[t.w[0]] = t.w[1]
            for s, c in t.r.items():
                if deps.get(s, 0) < c:
                    deps[s] = c
        for s, c in deps.items():
            if s is e and e.name == "pe":
                continue
            if e.seen.get(s, 0) < c:
                e.eng.wait_ge(s.sem, c)
                e.seen[s] = c

    def op(self, e, fn, R=(), W=(), inc=True):
        self._waits(e, R, W)
        ins = fn(e.eng)
        if inc:
            e.cnt += 1
            ins.then_inc(e.sem, 1)
            c = e.cnt
        else:
            c = e.cnt + 1
        for t in R:
            t.r[e] = c
        for t in W:
            t.w = (e, c)
            t.r = {}
        return ins

    def dma(self, q, out, in_, R=(), W=(), **kw):
        self._waits(q, R, W)
        lst = self.dsems[q.name]
        s = lst[self.dnext[q.name] % len(lst)]
        self.dnext[q.name] += 1
        q.eng.dma_start(out=out, in_=in_, **kw).then_inc(s.sem, 16)
        s.cnt += 16
        for t in R:
            t.r[s] = s.cnt
        for t in W:
            t.w = (s, s.cnt)
            t.r = {}

    def barrier(self):
        srcs = list(self.engs) + [s for l in self.dsems.values() for s in l] + [self.ccsem]
        for e in self.engs:
            for s in srcs:
                if s is e:
                    continue
                if s.cnt > 0 and e.seen.get(s, 0) < s.cnt:
                    e.eng.wait_ge(s.sem, s.cnt)
                    e.seen[s] = s.cnt


def build(NP, NS, upto=99, dbg=False):
    import os
    fast = bool(os.environ.get("KFAST"))
    NT = NP + NS
    NTOK = 512 * NT
    NCH = 4 * NT
    NCHP = 4 * NP
    NCHS = 4 * NS
    TP = 512 * NP
    nc = bass.Bass("TRN2", target_bir_lowering=False)

    def din(name, shape, dt=F32):
        return nc.dram_tensor(name, list(shape), dt, kind="ExternalInput").ap()

    xin = din("xin", [NTOK, D])
    mem2 = din("mem2", [2 * NMEM, D])
    rot = din("rot", [NTOK, 128])
    cst = din("cst", [128, 4 + 256 + 128])
    sel = din("sel", [128, 48])
    norm_g = din("norm_g", [2, 8, D])
    w_gu = [din("ffn1_w_gu", [2, D, 2 * DFF]), din("ffn2_w_gu", [2, D, 2 * DFF])]
    w_dn = [din("ffn1_w_down", [2, DFF, D]), din("ffn2_w_down", [2, DFF, D])]
    w_in = din("w_in", [2, D, INC])
    conv_dw_w = din("conv_dw_w", [2, CW, CONV])
    conv_dw_b = din("conv_dw_b", [2, CONV])
    conv_ln_g = din("conv_ln_g", [2, CONV])
    conv_ln_b = din("conv_ln_b", [2, CONV])
    conv_w_pw = din("conv_w_pw", [2, CONV, D])
    ret_decay_fwd = din("ret_decay_fwd", [2, H])
    ret_decay_bwd = din("ret_decay_bwd", [2, H])
    ret_gn_g = din("ret_gn_g", [2, RET])
    ret_w_out = din("ret_w_out", [2, RET, D])
    gate_b = din("gate_b", [2, 2 * D])
    w_mix_out = din("w_mix_out", [2, D, D])
    mem_norm_g = din("mem_norm_g", [2, D])
    xattn_w_q = din("xattn_w_q", [2, D, D])
    xattn_w_kv = din("xattn_w_kv", [2, D, 2 * D])
    xattn_w_o = din("xattn_w_o", [2, D, D])
    yout = nc.dram_tensor("yout", [NTOK, D], F32, kind="ExternalOutput").ap()

    X = nc.dram_tensor("Xs", [NTOK, D], F32).ap()
    HALO = 16
    UT = nc.dram_tensor("UTs", [CONV, NTOK], BF16).ap()
    GT = nc.dram_tensor("GTs", [RET, NTOK], BF16).ap()
    GA = nc.dram_tensor("GAs", [2 * D, NTOK], BF16).ap()
    QT = nc.dram_tensor("QTs", [RET, NTOK], BF16).ap()
    KT = nc.dram_tensor("KTs", [RET, NTOK], BF16).ap()
    QFB = nc.dram_tensor("QFBs", [H * 128, NTOK], BF16).ap()
    VV = nc.dram_tensor("VVs", [NTOK, RET], BF16).ap()
    KV = nc.dram_tensor("KVs", [128, NCH, H * DH], F32).ap()
    SBD = nc.dram_tensor("SBDs", [128, NCH, H * DH], BF16).ap()
    CSEND = nc.dram_tensor("csend", [128, 640], F32)
    CRECV = nc.dram_tensor("crecv", [NCORE * 128, 640], F32)

    es = contextlib.ExitStack()
    with es:
        fw = FW(nc, es)
        PE, ACT, DVE, POOL, SP = fw.pe, fw.act, fw.dve, fw.pool, fw.sp
        op, dma = fw.op, fw.dma

        _uid = [0]

        def sb(name, shape, dt, st=es):
            _uid[0] += 1
            return st.enter_context(nc.sbuf_tensor(f"{name}_{_uid[0]}", list(shape), dt))

        cst_sb = sb("cst_sb", [128, 388], F32)
        sel_sb = sb("sel_sb", [128, 48], F32)
        ident = sb("ident", [128, 128], BF16)
        epsb = sb("epsb", [128, 1], F32)
        ln8b = sb("ln8b", [128, 1], F32)
        t_c = T()
        dma(SP, cst_sb[:], cst, W=[t_c])
        dma(SP, sel_sb[:], sel, W=[t_c])
        t_c2 = T()
        op(DVE, lambda e: e.tensor_copy(out=ident[:], in_=cst_sb[:, 260:388]), R=[t_c], W=[t_c2])
        op(DVE, lambda e: e.memset(epsb[:], EPS), W=[t_c2])
        op(DVE, lambda e: e.memset(ln8b[:], float(np.log(0.125))), W=[t_c2])
        posv = cst_sb[:, 0:4]
        relp = cst_sb[:, 4:132]
        reln = cst_sb[:, 132:260]

        ps_es = contextlib.ExitStack()
        es.enter_context(ps_es)
        PS = [ps_es.enter_context(nc.psum_tensor(f"ps{i}", [128, 512], F32)) for i in range(8)]
        PST = [T(ps=True) for _ in range(8)]

        def ps_bf(i):
            return PS[i][:].bitcast(BF16)

        def bc_load(dst, src_row, t):
            dma(SP, dst, src_row.partition_broadcast(128), W=[t])

        def load_norm_T(src, t, xt, xt_t, hb, hb_t, hT, hT_t, gbc, gbc_t, ss, ss_t, rs, rs_t, load=True, nsub=4, row0=None):
            for s in range(nsub):
                r0 = (t * 512 + s * 128) if row0 is None else row0 + s * 128
                if load:
                    dma(SP, xt[:, s, :], src[r0:r0 + 128, :], W=[xt_t[s]])
            for s in range(nsub):
                op(ACT, lambda e, s=s: e.activation(out=hb[:, s % 2, :], in_=xt[:, s, :], func=AF.Square, accum_out=ss[:, s:s + 1]),
                   R=[xt_t[s]], W=[hb_t[s % 2], ss_t])
            op(ACT, lambda e: e.activation(out=rs[:, 0:nsub], in_=ss[:, 0:nsub], func=AF.Sqrt, scale=1.0 / D, bias=epsb[:]), R=[ss_t, t_c2], W=[rs_t])
            op(DVE, lambda e: e.reciprocal(out=rs[:, 0:nsub], in_=rs[:, 0:nsub]), R=[rs_t], W=[rs_t])
            for s in range(nsub):
                eng = DVE
                op(eng, lambda e, s=s: e.scalar_tensor_tensor(out=hb[:, s % 2, :], in0=xt[:, s, :], scalar=rs[:, s:s + 1], in1=gbc,
                                                               op0=ALU.mult, op1=ALU.mult), R=[xt_t[s], rs_t, gbc_t], W=[hb_t[s % 2]])
                b = s % 2
                for k in range(8):
                    op(PE, lambda e, k=k, s=s, b=b: e.transpose(out=ps_bf(b)[:, k * 128:(k + 1) * 128], in_=hb[:, s % 2, k * 128:(k + 1) * 128], identity=ident[:]),
                       R=[hb_t[s % 2], t_c2], W=[PST[b]], inc=(k == 7))
                op(ACT if s % 2 == 0 else DVE,
                   (lambda e, s=s, b=b: e.copy(out=hT[:, :, s * 128:(s + 1) * 128], in_=ps_bf(b).rearrange("p (k n) -> p k n", k=8))) if s % 2 == 0 else
                   (lambda e, s=s, b=b: e.tensor_copy(out=hT[:, :, s * 128:(s + 1) * 128], in_=ps_bf(b).rearrange("p (k n) -> p k n", k=8))),
                   R=[PST[b]], W=[hT_t])

        def post_norm_res(po, po_t, xt_s, xt_t_s, gbc, gbc_t, tmp, tmp_t, xo_s, xo_t_s, ss2, ss2_t, rs2, rs2_t, dst_rows):
            for hf in range(2):
                op(ACT, lambda e, hf=hf: e.activation(out=tmp[:, hf * 512:(hf + 1) * 512], in_=po[hf], func=AF.Square, accum_out=ss2[:, hf:hf + 1]),
                   R=[po_t[hf]], W=[tmp_t, ss2_t])
            op(DVE, lambda e: e.tensor_tensor(out=ss2[:, 2:3], in0=ss2[:, 0:1], in1=ss2[:, 1:2], op=ALU.add), R=[ss2_t], W=[ss2_t])
            op(ACT, lambda e: e.activation(out=rs2[:], in_=ss2[:, 2:3], func=AF.Sqrt, scale=1.0 / D, bias=epsb[:]), R=[ss2_t], W=[rs2_t])
            op(DVE, lambda e: e.reciprocal(out=rs2[:], in_=rs2[:]), R=[rs2_t], W=[rs2_t])
            for hf in range(2):
                op(DVE, lambda e, hf=hf: e.scalar_tensor_tensor(out=tmp[:, hf * 512:(hf + 1) * 512], in0=po[hf], scalar=rs2[:, 0:1], in1=gbc[:, hf * 512:(hf + 1) * 512],
                                                                 op0=ALU.mult, op1=ALU.mult), R=[po_t[hf], rs2_t, gbc_t], W=[tmp_t])
            op(POOL, lambda e: e.tensor_tensor(out=xo_s, in0=tmp[:], in1=xt_s, op=ALU.add), R=[tmp_t, xt_t_s], W=[xo_t_s])
            dma(SP, dst_rows, xo_s, R=[xo_t_s])

        def phase_ffn(l, which, src, dst):
            fw.barrier()
            with contextlib.ExitStack() as st:
                wgu = sb("wgu", [128, 8, 2 * DFF], BF16, st)
                wd = sb("wd", [128, NJ, D], BF16, st)
                xt = sb("xt", [128, 4, D], F32, st)
                hb = sb("hb", [128, 2, D], BF16, st)
                hT = sb("hT", [128, 8, 512], BF16, st)
                actT = sb("actT", [128, NJ, 512], BF16, st)
                g0 = sb("g0", [128, D], F32, st)
                g1 = sb("g1", [128, D], F32, st)
                sg = sb("sg", [128, 2, 512], F32, st)
                tmp = sb("tmp", [128, 1, D], F32, st)
                ss = sb("ss", [128, 4], F32, st)
                rs = sb("rs", [128, 4], F32, st)
                ss2 = sb("ss2", [128, 2, 4], F32, st)
                rs2 = sb("rs2", [128, 2, 1], F32, st)
                wgu_t = [T() for _ in range(8)]
                wd_t = [T() for _ in range(NJ)]
                xt_t = [T() for _ in range(4)]
                hb_t = [T() for _ in range(4)]
                hT_t, ss_t, rs_t, g0_t, g1_t = T(), T(), T(), T(), T()
                act_t = [T() for _ in range(NJ)]
                sg_t = [T(), T()]
                tmp_t = [T(), T()]
                xo_t = [T(), T()]
                ss2_t = [T(), T()]
                rs2_t = [T(), T()]
                ni = 0 if which == 0 else 6
                bc_load(g0[:], norm_g[l, ni:ni + 1, :], g0_t)
                bc_load(g1[:], norm_g[l, ni + 1:ni + 2, :], g1_t)
                op(DVE, lambda e: e.tensor_scalar(out=g1[:], in0=g1[:], scalar1=0.5, scalar2=None, op0=ALU.mult), R=[g1_t], W=[g1_t])
                for k in range(8):
                    dma(POOL, wgu[:, k, :], w_gu[which][l, k * 128:(k + 1) * 128, :], W=[wgu_t[k]])
                for j in range(NJ):
                    dma(POOL, wd[:, j, :], w_dn[which][l, j * 128:(j + 1) * 128, :], W=[wd_t[j]])
                cnt = 0
                for t in range(NT):
                    load_norm_T(src, t, xt, xt_t, hb, hb_t, hT, hT_t, g0[:], g0_t, ss, ss_t, rs, rs_t)
                    for j in range(NJ):
                        b = 2 + 2 * (j % 2)
                        for gi, col in enumerate((j * 128, DFF + j * 128)):
                            for k in range(8):
                                op(PE, lambda e, k=k, col=col, bb=b + gi: e.matmul(PS[bb][:], lhsT=wgu[:, k, col:col + 128], rhs=hT[:, k, :], start=(k == 0), stop=(k == 7)),
                                   R=[wgu_t[k], hT_t], W=[PST[b + gi]], inc=(k == 7))
                        op(ACT, lambda e, b=b, j=j: e.activation(out=sg[:, j % 2, :], in_=PS[b][:], func=AF.Silu), R=[PST[b]], W=[sg_t[j % 2]])
                        op(DVE, lambda e, b=b, j=j: e.tensor_tensor(out=actT[:, j, :], in0=sg[:, j % 2, :], in1=PS[b + 1][:], op=ALU.mult),
                           R=[sg_t[j % 2], PST[b + 1]], W=[act_t[j]])
                    for s in range(4):
                        pb = 2 + 2 * (cnt % 3)
                        q = 0
                        cnt += 1
                        for hf in range(2):
                            for j in range(NJ):
                                op(PE, lambda e, j=j, hf=hf, s=s, pb=pb: e.matmul(PS[pb + hf][:], lhsT=actT[:, j, s * 128:(s + 1) * 128], rhs=wd[:, j, hf * 512:(hf + 1) * 512],
                                                                                  start=(j == 0), stop=(j == NJ - 1)),
                                   R=[act_t[j], wd_t[j]], W=[PST[pb + hf]], inc=(j == NJ - 1))
                        r0 = t * 512 + s * 128
                        post_norm_res([PS[pb][:], PS[pb + 1][:]], [PST[pb], PST[pb + 1]], xt[:, s, :], xt_t[s], g1[:], g1_t, tmp[:, q, :], tmp_t[q],
                                      tmp[:, q, :], tmp_t[q], ss2[:, q, :], ss2_t[q], rs2[:, q, :], rs2_t[q], dst[r0:r0 + 128, :])
                fw.barrier()


        lgfb = sb("lgfb", [128, 8], F32)
        lgf = sb("lgf", [128, 8], F32)
        lgb = sb("lgb", [128, 8], F32)
        tabs = sb("tabs", [128, 4, 8], F32)
        Gd = sb("Gd", [128, 8], F32)
        DT = sb("DT", [128, 8, 128], F32)
        dtmp = sb("dtmp", [128, 2, 128], F32)
        uh = sb("uh", [128, 4, 2, 15], F32)
        hlr = sb("hlr", [128, 2, 4, 15], BF16)
        tab_t, uh_t, hlr_t = T(), T(), T()

        def ret_tables(l):
            dma(SP, lgf[:], ret_decay_fwd[l:l + 1, :].partition_broadcast(128), W=[tab_t])
            dma(SP, lgb[:], ret_decay_bwd[l:l + 1, :].partition_broadcast(128), W=[tab_t])
            dma(SP, lgfb[0:64, :], ret_decay_fwd[l:l + 1, :].partition_broadcast(64), W=[tab_t])
            dma(SP, lgfb[64:128, :], ret_decay_bwd[l:l + 1, :].partition_broadcast(64), W=[tab_t])
            for tt in (lgf, lgb, lgfb):
                op(ACT, lambda e, tt=tt: e.activation(out=tt[:], in_=tt[:], func=AF.Exp), R=[tab_t], W=[tab_t])
                op(DVE, lambda e, tt=tt: e.tensor_scalar(out=tt[:], in0=tt[:], scalar1=-1.0, scalar2=None, op0=ALU.mult), R=[tab_t], W=[tab_t])
            for i, (src, hasb) in enumerate(((lgf, False), (lgb, False), (lgf, True), (lgb, True))):
                if hasb:
                    op(ACT, lambda e, i=i, src=src: e.activation(out=tabs[:, i, :], in_=src[:], func=AF.Exp, scale=posv[:, i:i + 1], bias=ln8b[:]), R=[tab_t, t_c, t_c2], W=[tab_t])
                else:
                    op(ACT, lambda e, i=i, src=src: e.activation(out=tabs[:, i, :], in_=src[:], func=AF.Exp, scale=posv[:, i:i + 1]), R=[tab_t, t_c, t_c2], W=[tab_t])
            op(ACT, lambda e: e.activation(out=Gd[:], in_=lgfb[:], func=AF.Exp, scale=128.0), R=[tab_t], W=[tab_t])
            for h in range(H):
                op(DVE, lambda e, h=h: e.tensor_scalar(out=dtmp[:, 0, :], in0=relp, scalar1=lgf[:, h:h + 1], scalar2=None, op0=ALU.mult), R=[tab_t, t_c], W=[tab_t])
                op(DVE, lambda e, h=h: e.scalar_tensor_tensor(out=dtmp[:, 1, :], in0=reln, scalar=lgb[:, h:h + 1], in1=dtmp[:, 0, :], op0=ALU.mult, op1=ALU.add), R=[tab_t, t_c], W=[tab_t])
                op(ACT, lambda e, h=h: e.activation(out=DT[:, h, :], in_=dtmp[:, 1, :], func=AF.Exp, bias=ln8b[:]), R=[tab_t, t_c2], W=[tab_t])

        def phase_inproj(l):
            fw.barrier()
            ret_tables(l)
            with contextlib.ExitStack() as st:
                win = sb("win", [128, 8, INC], BF16, st)
                xt = sb("xt", [128, 4, D], F32, st)
                hb = sb("hb", [128, 2, D], BF16, st)
                hT = sb("hT", [128, 8, 512], BF16, st)
                g2 = sb("g2", [128, D], F32, st)
                gbias = sb("gbias", [128, 16], F32, st)
                ss = sb("ss", [128, 4], F32, st)
                rs = sb("rs", [128, 4], F32, st)
                sig = sb("sig", [128, 2, 512], F32, st)
                ust = sb("ust", [128, 4, 512], BF16, st)
                gst = sb("gst", [128, 4, 512], BF16, st)
                gast = sb("gast", [128, 16, 512], BF16, st)
                qkst = sb("qkst", [128, 16, 512], BF16, st)
                kvst = sb("kvst", [128, 4, 512], F32, st)
                vb = sb("vb", [128, 4, 512], BF16, st)
                rt = sb("rt", [128, 4, 128], F32, st)
                qr = sb("qr", [128, 2, 512], F32, st)
                ra = sb("ra", [128, 2, 512], F32, st)
                qrb = sb("qrb", [128, 2, 512], BF16, st)
                qfb = sb("qfb", [128, 2, 8, 128], BF16, st)
                win_t = [T() for _ in range(8)]
                xt_t = [T() for _ in range(4)]
                hb_t = [T(), T()]
                hT_t, ss_t, rs_t, g2_t, gb_t = T(), T(), T(), T(), T()
                sig_t = [T(), T()]
                ust_t, gst_t, gast_t, qkst_t, kvst_t = T(), T(), T(), T(), T()
                vb_t = [T() for _ in range(4)]
                rt_t = T()
                qr_t = [T(), T()]
                ra_t = [T(), T()]
                qrb_t = [T(), T()]
                qfb_t = [T(), T()]
                bc_load(g2[:], norm_g[l, 2:3, :], g2_t)
                dma(SP, gbias[:], gate_b[l].rearrange("(m p) -> p m", p=128), W=[gb_t], allow_slow_non_contiguous=True)
                for k in range(8):
                    dma(POOL, win[:, k, :], w_in[l, k * 128:(k + 1) * 128, :], W=[win_t[k]])
                for t in range(NT):
                    load_norm_T(X, t, xt, xt_t, hb, hb_t, hT, hT_t, g2[:], g2_t, ss, ss_t, rs, rs_t)
                    dma(SP, rt[:], rot[t * 512:(t + 1) * 512, :].rearrange("(s p) f -> p s f", p=128), W=[rt_t])

                    def fm(col, b):
                        for k in range(8):
                            op(PE, lambda e, k=k: e.matmul(PS[b][:], lhsT=win[:, k, col:col + 128], rhs=hT[:, k, :], start=(k == 0), stop=(k == 7)),
                               R=[win_t[k], hT_t], W=[PST[b]], inc=(k == 7))
                    for m in range(4):
                        fm(m * 128, 2)
                        fm(512 + m * 128, 3)
                        op(ACT, lambda e, m=m: e.activation(out=sig[:, m % 2, :], in_=PS[3][:], func=AF.Sigmoid), R=[PST[3]], W=[sig_t[m % 2]])
                        op(DVE, lambda e, m=m: e.tensor_tensor(out=ust[:, m, :], in0=PS[2][:], in1=sig[:, m % 2, :], op=ALU.mult), R=[PST[2], sig_t[m % 2]], W=[ust_t])
                        if t == NP:
                            op(DVE, lambda e, m=m: e.tensor_tensor(out=uh[:, m, 0, :], in0=PS[2][:, 0:15], in1=sig[:, m % 2, 0:15], op=ALU.mult), R=[PST[2], sig_t[m % 2]], W=[uh_t])
                        if t == NT - 1:
                            op(DVE, lambda e, m=m: e.tensor_tensor(out=uh[:, m, 1, :], in0=PS[2][:, 497:512], in1=sig[:, m % 2, 497:512], op=ALU.mult), R=[PST[2], sig_t[m % 2]], W=[uh_t])
                    dma(SP, UT[:, t * 512:(t + 1) * 512].rearrange("(m p) n -> p m n", p=128), ust[:], R=[ust_t])
                    for m in range(4):
                        b = 2 + m % 2
                        fm(2560 + m * 128, b)
                        op(ACT, lambda e, m=m, b=b: e.activation(out=gst[:, m, :], in_=PS[b][:], func=AF.Silu), R=[PST[b]], W=[gst_t])
                    dma(SP, GT[:, t * 512:(t + 1) * 512].rearrange("(m p) n -> p m n", p=128), gst[:], R=[gst_t])
                    for m in range(16):
                        b = 2 + m % 2
                        fm(3072 + m * 128, b)
                        op(ACT, lambda e, m=m, b=b: e.activation(out=gast[:, m, :], in_=PS[b][:], func=AF.Sigmoid, bias=gbias[:, m:m + 1]), R=[PST[b], gb_t], W=[gast_t])
                    dma(SP, GA[:, t * 512:(t + 1) * 512].rearrange("(m p) n -> p m n", p=128), gast[:], R=[gast_t])
                    for s in range(4):
                        for qi, col in enumerate((1024, 1536, 2048)):
                            b = 4 + qi
                            for k in range(8):
                                op(PE, lambda e, k=k, col=col, b=b, s=s: e.matmul(PS[b][:], lhsT=hT[:, k, s * 128:(s + 1) * 128], rhs=win[:, k, col:col + 512], start=(k == 0), stop=(k == 7)),
                                   R=[win_t[k], hT_t], W=[PST[b]], inc=(k == 7))
                        op(ACT, lambda e, s=s: e.copy(out=vb[:, s, :], in_=PS[6][:]), R=[PST[6]], W=[vb_t[s]])
                        for qi in range(2):
                            b = 4 + qi
                            pq = PS[b][:].rearrange("p (h two f) -> p h two f", h=8, two=2)
                            cc = rt[:, s, 0:64].unsqueeze(1).to_broadcast([128, 8, 64])
                            nsn = rt[:, s, 64:96].unsqueeze(1).to_broadcast([128, 8, 32])
                            psn = rt[:, s, 96:128].unsqueeze(1).to_broadcast([128, 8, 32])
                            qr3 = qr[:, qi, :].rearrange("p (h f) -> p h f", h=8)
                            ra4 = ra[:, qi, :].rearrange("p (h two f) -> p h two f", h=8, two=2)
                            op(DVE, lambda e, b=b, qr3=qr3, cc=cc: e.tensor_tensor(out=qr3, in0=PS[b][:].rearrange("p (h f) -> p h f", h=8), in1=cc, op=ALU.mult), R=[PST[b], rt_t], W=[qr_t[qi]])
                            op(DVE, lambda e, pq=pq, ra4=ra4, nsn=nsn: e.tensor_tensor(out=ra4[:, :, 0, :], in0=pq[:, :, 1, :], in1=nsn, op=ALU.mult), R=[PST[b], rt_t], W=[ra_t[qi]])
                            op(DVE, lambda e, pq=pq, ra4=ra4, psn=psn: e.tensor_tensor(out=ra4[:, :, 1, :], in0=pq[:, :, 0, :], in1=psn, op=ALU.mult), R=[PST[b], rt_t], W=[ra_t[qi]])
                            op(POOL, lambda e, qi=qi: e.tensor_tensor(out=qr[:, qi, :], in0=qr[:, qi, :], in1=ra[:, qi, :], op=ALU.add), R=[qr_t[qi], ra_t[qi]], W=[qr_t[qi]])
                            op(ACT, lambda e, qi=qi: e.copy(out=qrb[:, qi, :], in_=qr[:, qi, :]), R=[qr_t[qi]], W=[qrb_t[qi]])
                            for fbi in range(2):
                                tb = tabs[:, 2 * qi + fbi, :].unsqueeze(2).to_broadcast([128, 8, 64])
                                dst = qfb[:, qi, :, fbi * 64:(fbi + 1) * 64]
                                op(DVE if fbi == 0 else POOL, lambda e, dst=dst, tb=tb, qr3=qr3: e.tensor_tensor(out=dst, in0=qr3, in1=tb, op=ALU.mult), R=[qr_t[qi], tab_t], W=[qfb_t[qi]])
                        for i in range(8):
                            qi, pr = i // 4, i % 4
                            op(PE, lambda e, i=i, qi=qi, pr=pr: e.transpose(out=ps_bf(0)[:, i * 128:(i + 1) * 128], in_=qrb[:, qi, pr * 128:(pr + 1) * 128], identity=ident[:]),
                               R=[qrb_t[qi]], W=[PST[0]], inc=(i == 7))
                        for h in range(8):
                            op(PE, lambda e, h=h: e.transpose(out=ps_bf(1)[:, h * 128:(h + 1) * 128], in_=qfb[:, 0, h, :], identity=ident[:]),
                               R=[qfb_t[0]], W=[PST[1]], inc=(h == 7))
                        op(ACT, lambda e, s=s: e.copy(out=qkst[:, 0:8, s * 128:(s + 1) * 128], in_=ps_bf(0).rearrange("p (k n) -> p k n", k=8)), R=[PST[0]], W=[qkst_t])
                        op(DVE, lambda e, s=s: e.tensor_copy(out=qkst[:, 8:16, s * 128:(s + 1) * 128], in_=ps_bf(1).rearrange("p (k n) -> p k n", k=8)), R=[PST[1]], W=[qkst_t])
                        for h in range(8):
                            op(PE, lambda e, h=h, s=s: e.matmul(PS[7][:, h * 64:(h + 1) * 64], lhsT=qfb[:, 1, h, :], rhs=vb[:, s, h * 64:(h + 1) * 64], start=True, stop=True),
                               R=[qfb_t[1], vb_t[s]], W=[PST[7]], inc=(h == 7))
                        op(ACT, lambda e, s=s: e.copy(out=kvst[:, s, :], in_=PS[7][:]), R=[PST[7]], W=[kvst_t])
                    dma(SP, QT[:, t * 512:(t + 1) * 512].rearrange("(m p) n -> p m n", p=128), qkst[:, 0:4, :], R=[qkst_t])
                    dma(SP, KT[:, t * 512:(t + 1) * 512].rearrange("(m p) n -> p m n", p=128), qkst[:, 4:8, :], R=[qkst_t])
                    dma(SP, QFB[:, t * 512:(t + 1) * 512].rearrange("(m p) n -> p m n", p=128), qkst[:, 8:16, :], R=[qkst_t])
                    dma(SP, VV[t * 512:(t + 1) * 512, :].rearrange("(s p) f -> p s f", p=128), vb[:], R=vb_t)
                    dma(SP, KV[:, 4 * t:4 * t + 4, :], kvst[:], R=[kvst_t])
                fw.barrier()

        def phase_scan(l, SBD):
            fw.barrier()
            with contextlib.ExitStack() as st:
                Sbf = sb("Sbf", [128, NCH, 512], BF16, st)
                Sbf_t = T()
                S32 = sb("S32", [128, NCH, 512], F32, st)
                stmp = sb("stmp", [128, 2, 512], F32, st)
                rc = sb("rc", [128, NCORE, 640], F32, st)
                Wfb = sb("Wfb", [128, NCORE, 8], F32, st)
                ini = sb("ini", [128, 512], F32, st)
                hacc = sb("hacc", [128, 2, 4, 15], F32, st)
                S_t, rc_t, W_t, ini_t, hacc_t = T(), T(), T(), T(), T()
                stmp_t = [T(), T()]
                for c0 in range(0, NCH, 8):
                    dma(SP, S32[:, c0:c0 + 8, :], KV[:, c0:c0 + 8, :], W=[S_t])
                Gb = Gd[:].unsqueeze(2).to_broadcast([128, 8, 64])

                def v3(ap):
                    return ap.rearrange("p (h e) -> p h e", h=8)

                def scan(lo, hi):
                    for n in range(lo + 1, hi):
                        op(DVE, lambda e, n=n: e.tensor_tensor(out=v3(stmp[0:64, 0, :]), in0=v3(S32[0:64, n - 1, :]), in1=Gb[0:64], op=ALU.mult), R=[S_t, tab_t], W=[stmp_t[0]])
                        op(DVE, lambda e, n=n: e.tensor_tensor(out=S32[0:64, n, :], in0=S32[0:64, n, :], in1=stmp[0:64, 0, :], op=ALU.add), R=[stmp_t[0], S_t], W=[S_t])
                    for n in range(hi - 2, lo - 1, -1):
                        op(POOL, lambda e, n=n: e.tensor_tensor(out=v3(stmp[64:128, 1, :]), in0=v3(S32[64:128, n + 1, :]), in1=Gb[64:128], op=ALU.mult), R=[S_t, tab_t], W=[stmp_t[1]])
                        op(POOL, lambda e, n=n: e.tensor_tensor(out=S32[64:128, n, :], in0=S32[64:128, n, :], in1=stmp[64:128, 1, :], op=ALU.add), R=[stmp_t[1], S_t], W=[S_t])
                scan(NCHP, NCH)
                cs = CSEND.ap()
                t_cs = T()
                dma(SP, cs[0:64, 0:512], S32[0:64, NCH - 1, :], R=[S_t], W=[t_cs])
                dma(SP, cs[64:128, 0:512], S32[64:128, NCHP, :], R=[S_t], W=[t_cs])
                dma(SP, cs[:, 512:632].rearrange("p (m a f) -> p m a f", m=4, a=2), uh[:], R=[uh_t], W=[t_cs])
                fw.barrier()
                POOL.eng.collective_compute("AllGather", ALU.bypass, replica_groups=[list(range(NCORE))],
                                            ins=[CSEND.ap().opt()], outs=[CRECV.ap().opt()]).then_inc(fw.ccsem.sem)
                fw.ccsem.cnt += 1
                scan(0, NCHP)
                fw.barrier()
                dma(SP, rc[:], CRECV.ap().rearrange("(r p) n -> p r n", p=128), W=[rc_t])
                op(DVE, lambda e: e.memset(Sbf[0:64, 0, :], 0.0), W=[Sbf_t])
                op(POOL, lambda e: e.memset(Sbf[64:128, NCHP - 1, :], 0.0), W=[Sbf_t])
                op(ACT, lambda e: e.copy(out=Sbf[0:64, 1:NCHP, :], in_=S32[0:64, 0:NCHP - 1, :]), R=[S_t], W=[Sbf_t])
                op(ACT, lambda e: e.copy(out=Sbf[64:128, 0:NCHP - 1, :], in_=S32[64:128, 1:NCHP, :]), R=[S_t], W=[Sbf_t])
                for cp in range(NCORE):
                    op(ACT, lambda e, cp=cp: e.activation(out=Wfb[:, cp, :], in_=lgfb[:], func=AF.Exp, scale=sel_sb[:, 24 + cp:25 + cp]), R=[tab_t, t_c], W=[W_t])
                    op(DVE, lambda e, cp=cp: e.tensor_scalar(out=Wfb[:, cp, :], in0=Wfb[:, cp, :], scalar1=sel_sb[:, 16 + cp:17 + cp], scalar2=None, op0=ALU.mult), R=[W_t, t_c], W=[W_t])
                op(DVE, lambda e: e.memset(ini[:], 0.0), W=[ini_t])
                op(DVE, lambda e: e.memset(hacc[:], 0.0), W=[hacc_t])
                for cp in range(NCORE):
                    op(DVE, lambda e, cp=cp: e.tensor_tensor(out=v3(stmp[:, 0, :]), in0=v3(rc[:, cp, 0:512]), in1=Wfb[:, cp, :].unsqueeze(2).to_broadcast([128, 8, 64]), op=ALU.mult),
                       R=[rc_t, W_t], W=[stmp_t[0]])
                    op(DVE, lambda e: e.tensor_tensor(out=ini[:], in0=ini[:], in1=stmp[:, 0, :], op=ALU.add), R=[stmp_t[0], ini_t], W=[ini_t])
                    rch = rc[:, cp, 512:632].rearrange("p (m a f) -> p m a f", m=4, a=2)
                    op(DVE, lambda e, cp=cp, rch=rch: e.scalar_tensor_tensor(out=hacc[:, 0, :, :], in0=rch[:, :, 1, :], scalar=sel_sb[:, cp:cp + 1], in1=hacc[:, 0, :, :], op0=ALU.mult, op1=ALU.add),
                       R=[rc_t, t_c, hacc_t], W=[hacc_t])
                    op(DVE, lambda e, cp=cp, rch=rch: e.scalar_tensor_tensor(out=hacc[:, 1, :, :], in0=rch[:, :, 0, :], scalar=sel_sb[:, 8 + cp:9 + cp], in1=hacc[:, 1, :, :], op0=ALU.mult, op1=ALU.add),
                       R=[rc_t, t_c, hacc_t], W=[hacc_t])
                op(DVE, lambda e: e.tensor_copy(out=hlr[:], in_=hacc[:]), R=[hacc_t], W=[hlr_t])
                for i in range(NCHS):
                    n = NCHP + i
                    if i == 0:
                        op(DVE, lambda e, n=n: e.tensor_copy(out=Sbf[0:64, n, :], in_=ini[0:64, :]), R=[ini_t], W=[Sbf_t])
                    else:
                        op(DVE, lambda e, n=n: e.tensor_tensor(out=Sbf[0:64, n, :], in0=S32[0:64, n - 1, :], in1=ini[0:64, :], op=ALU.add), R=[ini_t, S_t], W=[Sbf_t])
                    if i < NCHS - 1:
                        op(DVE, lambda e: e.tensor_tensor(out=v3(ini[0:64, :]), in0=v3(ini[0:64, :]), in1=Gb[0:64], op=ALU.mult), R=[ini_t, tab_t], W=[ini_t])
                for i in range(NCHS - 1, -1, -1):
                    n = NCHP + i
                    if i == NCHS - 1:
                        op(POOL, lambda e, n=n: e.tensor_copy(out=Sbf[64:128, n, :], in_=ini[64:128, :]), R=[ini_t], W=[Sbf_t])
                    else:
                        op(POOL, lambda e, n=n: e.tensor_tensor(out=Sbf[64:128, n, :], in0=S32[64:128, n + 1, :], in1=ini[64:128, :], op=ALU.add), R=[ini_t, S_t], W=[Sbf_t])
                    if i > 0:
                        op(POOL, lambda e: e.tensor_tensor(out=v3(ini[64:128, :]), in0=v3(ini[64:128, :]), in1=Gb[64:128], op=ALU.mult), R=[ini_t, tab_t], W=[ini_t])
                for c0 in range(0, NCH, 8):
                    dma(SP, SBD[:, c0:c0 + 8, :], Sbf[:, c0:c0 + 8, :], R=[Sbf_t])
                fw.barrier()


        def phase_mixer(l, SBD):
            fw.barrier()
            with contextlib.ExitStack() as st:
                diag = sb("diag", [128, 4, CW, 128], BF16, st)
                dww = sb("dww", [128, 4, CW], F32, st)
                cpar = sb("cpar", [128, 4, 4], F32, st)
                ones = sb("ones", [128, 128], BF16, st)
                wpw = sb("wpw", [128, 4, D], BF16, st)
                wout = sb("wout", [128, 4, D], BF16, st)
                wmix = sb("wmix", [128, 8, D], BF16, st)
                g3 = sb("g3", [128, D], F32, st)
                ubuf = sb("ubuf", [128, 4, 544], BF16, st)
                gt = sb("gt", [128, 4, 512], BF16, st)
                ga = sb("ga", [128, 16, 512], BF16, st)
                qt = sb("qt", [64, 8, 512], BF16, st)
                kt = sb("kt", [64, 8, 512], BF16, st)
                qf = sb("qf", [128, 8, 512], BF16, st)
                vt = sb("vt", [128, 4, 512], BF16, st)
                sbt = sb("sbt", [128, 4, 512], BF16, st)
                xs_ = sb("xs_", [128, 2, D], F32, st)
                cvo = sb("cvo", [128, 4, 512], F32, st)
                cb16 = sb("cb16", [128, 4, 512], BF16, st)
                sq16 = sb("sq16", [128, 4, 512], BF16, st)
                swu = sb("swu", [128, 4, 512], BF16, st)
                stt = sb("stt", [128, 3, 512], F32, st)
                pt = sb("pt", [128, 8, 128], BF16, st)
                ysq = sb("ysq", [128, 512], F32, st)
                yst = sb("yst", [128, 4, 8], F32, st)
                ync = sb("ync", [128, 512], F32, st)
                ynb = sb("ynb", [128, 512], BF16, st)
                yg = sb("yg", [128, 4, 512], BF16, st)
                t1 = sb("t1", [128, 2, 512], F32, st)
                mixed = sb("mixed", [128, 8, 512], BF16, st)
                tmp = sb("tmp", [128, 1, D], F32, st)
                ss2 = sb("ss2", [128, 1, 4], F32, st)
                rs2 = sb("rs2", [128, 1, 1], F32, st)
                w_t, cp_t, g3_t, ub_t, gt_t, ga_t, qt_t, kt_t, qf_t, vt_t, sbt_t = [T() for _ in range(11)]
                xs_t = [T(), T()]
                cvo_t, cb_t, sq_t, swu_t, stt_t, pt_t, ysq_t, yst_t, ync_t, ynb_t, yg_t, mixed_t, tmp_t, ss2_t, rs2_t = [T() for _ in range(15)]
                t1_t = [T(), T()]
                bc_load(g3[:], norm_g[l, 3:4, :], g3_t)
                wnat = sb("wnat", [CW, CONV], F32, st)
                wnatb = sb("wnatb", [CW, CONV], BF16, st)
                wn_t = T()
                dma(SP, wnat[:], conv_dw_w[l], W=[wn_t])
                op(DVE, lambda e: e.tensor_copy(out=wnatb[:], in_=wnat[:]), R=[wn_t], W=[wn_t])
                for m in range(4):
                    op(PE, lambda e, m=m: e.matmul(PS[0][:, m * 32:m * 32 + CW], lhsT=wnatb[:, m * 128:(m + 1) * 128], rhs=ident[0:CW, 0:CW], start=True, stop=True),
                       R=[wn_t, t_c2], W=[PST[0]], inc=(m == 3))
                op(DVE, lambda e: e.tensor_copy(out=dww[:], in_=PS[0][:, 0:128].rearrange("p (m t) -> p m t", m=4)[:, :, 0:CW]), R=[PST[0]], W=[cp_t])
                for i, src in enumerate((conv_dw_b, conv_ln_g, conv_ln_b, ret_gn_g)):
                    dma(SP, cpar[:, i, :], src[l].rearrange("(m p) -> p m", p=128), W=[cp_t], allow_slow_non_contiguous=True)
                dma(POOL, wpw[:], conv_w_pw[l].rearrange("(m p) n -> p m n", p=128), W=[w_t])
                dma(POOL, wout[:], ret_w_out[l].rearrange("(m p) n -> p m n", p=128), W=[w_t])
                dma(POOL, wmix[:], w_mix_out[l].rearrange("(m p) n -> p m n", p=128), W=[w_t])
                op(DVE, lambda e: e.memset(ones[:], 1.0 / CONV), W=[w_t])
                for m in range(4):
                    for tp in range(CW):
                        op(DVE if (tp % 2 == 0) else POOL, lambda e, m=m, tp=tp: e.tensor_scalar(out=diag[:, m, tp, :], in0=cst_sb[:, 260:388], scalar1=dww[:, m, tp:tp + 1], scalar2=None, op0=ALU.mult),
                           R=[cp_t, t_c], W=[w_t])
                for t in range(NT):
                    c0, c1 = t * 512, (t + 1) * 512
                    fm3 = lambda ap: ap.rearrange("(m p) n -> p m n", p=128)
                    dma(SP, ubuf[:, :, 16:528], fm3(UT[:, c0:c1]), W=[ub_t])
                    if t == 0:
                        op(POOL, lambda e: e.memset(ubuf[:, :, 0:16], 0.0), W=[ub_t])
                    elif t == NP:
                        op(POOL, lambda e: e.tensor_copy(out=ubuf[:, :, 1:16], in_=hlr[:, 0, :, :]), R=[hlr_t], W=[ub_t])
                    else:
                        dma(SP, ubuf[:, :, 0:16], fm3(UT[:, c0 - 16:c0]), W=[ub_t], allow_slow_non_contiguous=True)
                    if t == NP - 1:
                        op(POOL, lambda e: e.memset(ubuf[:, :, 528:544], 0.0), W=[ub_t])
                    elif t == NT - 1:
                        op(POOL, lambda e: e.tensor_copy(out=ubuf[:, :, 528:543], in_=hlr[:, 1, :, :]), R=[hlr_t], W=[ub_t])
                    else:
                        dma(SP, ubuf[:, :, 528:544], fm3(UT[:, c1:c1 + 16]), W=[ub_t], allow_slow_non_contiguous=True)
                    dma(SP, gt[:], fm3(GT[:, c0:c1]), W=[gt_t])
                    dma(SP, ga[:], fm3(GA[:, c0:c1]), W=[ga_t])
                    dma(SP, qt[:], QT[:, c0:c1].rearrange("(h d) n -> d h n", d=64), W=[qt_t])
                    dma(SP, kt[:], KT[:, c0:c1].rearrange("(h d) n -> d h n", d=64), W=[kt_t])
                    dma(SP, qf[:], fm3(QFB[:, c0:c1]), W=[qf_t])
                    dma(SP, vt[:], VV[c0:c1, :].rearrange("(s p) f -> p s f", p=128), W=[vt_t])
                    dma(SP, sbt[:], SBD[:, 4 * t:4 * t + 4, :], W=[sbt_t])
                    for m in range(4):
                        for tp in range(CW):
                            op(PE, lambda e, m=m, tp=tp: e.matmul(PS[2 + m % 2][:], lhsT=diag[:, m, tp, :], rhs=ubuf[:, m, tp + 1:tp + 513], start=(tp == 0), stop=(tp == CW - 1)),
                               R=[w_t, ub_t], W=[PST[2 + m % 2]], inc=(tp == CW - 1))
                        b = 2 + m % 2
                        op(ACT, lambda e, m=m, b=b: e.activation(out=cvo[:, m, :], in_=PS[b][:], func=AF.Identity, bias=cpar[:, 0, m:m + 1]), R=[PST[b], cp_t], W=[cvo_t])
                        op(ACT, lambda e, m=m, b=b: e.activation(out=sq16[:, m, :], in_=PS[b][:], func=AF.Square, bias=cpar[:, 0, m:m + 1]), R=[PST[b], cp_t], W=[sq_t])
                        op(DVE, lambda e, m=m: e.tensor_copy(out=cb16[:, m, :], in_=cvo[:, m, :]), R=[cvo_t], W=[cb_t])
                    for m in range(4):
                        op(PE, lambda e, m=m: e.matmul(PS[4][:], lhsT=ones[:], rhs=cb16[:, m, :], start=(m == 0), stop=(m == 3)), R=[w_t, cb_t], W=[PST[4]], inc=(m == 3))
                    for m in range(4):
                        op(PE, lambda e, m=m: e.matmul(PS[5][:], lhsT=ones[:], rhs=sq16[:, m, :], start=(m == 0), stop=(m == 3)), R=[w_t, sq_t], W=[PST[5]], inc=(m == 3))
                    op(ACT, lambda e: e.activation(out=stt[:, 0, :], in_=PS[4][:], func=AF.Square), R=[PST[4]], W=[stt_t])
                    op(DVE, lambda e: e.tensor_tensor(out=stt[:, 1, :], in0=PS[5][:], in1=stt[:, 0, :], op=ALU.subtract), R=[PST[5], stt_t], W=[stt_t])
                    op(ACT, lambda e: e.activation(out=stt[:, 1, :], in_=stt[:, 1, :], func=AF.Sqrt, bias=epsb[:]), R=[stt_t, t_c2], W=[stt_t])
                    op(DVE, lambda e: e.reciprocal(out=stt[:, 1, :], in_=stt[:, 1, :]), R=[stt_t], W=[stt_t])
                    for m in range(4):
                        op(DVE, lambda e, m=m: e.tensor_tensor(out=cvo[:, m, :], in0=cvo[:, m, :], in1=PS[4][:], op=ALU.subtract), R=[cvo_t, PST[4]], W=[cvo_t])
                        op(POOL, lambda e, m=m: e.tensor_tensor(out=cvo[:, m, :], in0=cvo[:, m, :], in1=stt[:, 1, :], op=ALU.mult), R=[cvo_t, stt_t], W=[cvo_t])
                        op(ACT, lambda e, m=m: e.activation(out=swu[:, m, :], in_=cvo[:, m, :], func=AF.Silu, scale=cpar[:, 1, m:m + 1], bias=cpar[:, 2, m:m + 1]), R=[cvo_t, cp_t], W=[swu_t])
                    if dbg == "mixA":
                        continue
                    for s in range(4):
                        cs0 = s * 128
                        for h in range(8):
                            p0 = (h % 2) * 64
                            op(PE, lambda e, h=h, p0=p0, cs0=cs0: e.matmul(PS[h // 4][:, (h % 4) * 128:(h % 4 + 1) * 128], lhsT=kt[:, h, cs0:cs0 + 128], rhs=qt[:, h, cs0:cs0 + 128], start=True, stop=True),
                               R=[kt_t, qt_t], W=[PST[h // 4]], inc=(h % 4 == 3))
                        for bk in range(2):
                            op(DVE if bk == 0 else POOL if False else DVE, lambda e, bk=bk: e.tensor_tensor(out=pt[:, 4 * bk:4 * bk + 4, :], in0=PS[bk][:].rearrange("p (h i) -> p h i", h=4), in1=DT[:, 4 * bk:4 * bk + 4, :], op=ALU.mult),
                               R=[PST[bk], tab_t], W=[pt_t])
                        if dbg == "mixB1":
                            continue
                        for h in range(8):
                            op(PE, lambda e, h=h, s=s: e.matmul(PS[6][:, h * 64:(h + 1) * 64], lhsT=pt[:, h, :], rhs=vt[:, s, h * 64:(h + 1) * 64], start=True, stop=False),
                               R=[pt_t, vt_t], W=[PST[6]], inc=False)
                            op(PE, lambda e, h=h, s=s, cs0=cs0: e.matmul(PS[6][:, h * 64:(h + 1) * 64], lhsT=qf[:, h, cs0:cs0 + 128], rhs=sbt[:, s, h * 64:(h + 1) * 64], start=False, stop=True),
                               R=[qf_t, sbt_t], W=[PST[6]], inc=(h == 7))
                        y3 = PS[6][:].rearrange("p (h e) -> p h e", h=8)
                        op(ACT, lambda e: e.activation(out=ysq[:], in_=PS[6][:], func=AF.Square), R=[PST[6]], W=[ysq_t])
                        if dbg == "mixB2":
                            continue
                        op(DVE, lambda e, y3=y3: e.tensor_reduce(out=yst[:, 0, :], in_=y3, axis=AX.X, op=ALU.add), R=[PST[6]], W=[yst_t])
                        op(DVE, lambda e: e.tensor_reduce(out=yst[:, 1, :], in_=ysq[:].rearrange("p (h e) -> p h e", h=8), axis=AX.X, op=ALU.add), R=[ysq_t, yst_t], W=[yst_t])
                        op(DVE, lambda e: e.tensor_scalar(out=yst[:, 2, :], in0=yst[:, 0, :], scalar1=1.0 / DH, scalar2=None, op0=ALU.mult), R=[yst_t], W=[yst_t])
                        op(DVE, lambda e: e.tensor_tensor(out=yst[:, 0, :], in0=yst[:, 2, :], in1=yst[:, 2, :], op=ALU.mult), R=[yst_t], W=[yst_t])
                        op(DVE, lambda e: e.tensor_scalar(out=yst[:, 1, :], in0=yst[:, 1, :], scalar1=1.0 / DH, scalar2=None, op0=ALU.mult), R=[yst_t], W=[yst_t])
                        op(DVE, lambda e: e.tensor_tensor(out=yst[:, 3, :], in0=yst[:, 1, :], in1=yst[:, 0, :], op=ALU.subtract), R=[yst_t], W=[yst_t])
                        op(ACT, lambda e: e.activation(out=yst[:, 3, :], in_=yst[:, 3, :], func=AF.Sqrt, bias=epsb[:]), R=[yst_t, t_c2], W=[yst_t])
                        op(DVE, lambda e: e.reciprocal(out=yst[:, 3, :], in_=yst[:, 3, :]), R=[yst_t], W=[yst_t])
                        if dbg == "mixB3":
                            continue
                        op(DVE, lambda e, y3=y3: e.tensor_tensor(out=ync[:].rearrange("p (h e) -> p h e", h=8), in0=y3, in1=yst[:, 2, :].unsqueeze(2).to_broadcast([128, 8, 64]), op=ALU.subtract),
                           R=[PST[6], yst_t], W=[ync_t])
                        op(POOL, lambda e: e.tensor_tensor(out=ynb[:].rearrange("p (h e) -> p h e", h=8), in0=ync[:].rearrange("p (h e) -> p h e", h=8), in1=yst[:, 3, :].unsqueeze(2).to_broadcast([128, 8, 64]), op=ALU.mult),
                           R=[ync_t, yst_t], W=[ynb_t])
                        if dbg == "mixB4":
                            continue
                        for m in range(4):
                            op(PE, lambda e, m=m: e.transpose(out=ps_bf(7)[:, m * 128:(m + 1) * 128], in_=ynb[:, m * 128:(m + 1) * 128], identity=ident[:]), R=[ynb_t], W=[PST[7]], inc=(m == 3))
                        if dbg == "mixB5":
                            continue
                        for m in range(4):
                            op(DVE, lambda e, m=m, cs0=cs0: e.scalar_tensor_tensor(out=yg[:, m, cs0:cs0 + 128], in0=ps_bf(7)[:, m * 128:(m + 1) * 128], scalar=cpar[:, 3, m:m + 1], in1=gt[:, m, cs0:cs0 + 128],
                                                                                     op0=ALU.mult, op1=ALU.mult), R=[PST[7], cp_t, gt_t], W=[yg_t])
                    if dbg and dbg.startswith("mixB"):
                        continue
                    for o in range(8):
                        for m in range(4):
                            op(PE, lambda e, m=m, o=o: e.matmul(PS[2][:], lhsT=wpw[:, m, o * 128:(o + 1) * 128], rhs=swu[:, m, :], start=(m == 0), stop=(m == 3)), R=[w_t, swu_t], W=[PST[2]], inc=(m == 3))
                        for m in range(4):
                            op(PE, lambda e, m=m, o=o: e.matmul(PS[3][:], lhsT=wout[:, m, o * 128:(o + 1) * 128], rhs=yg[:, m, :], start=(m == 0), stop=(m == 3)), R=[w_t, yg_t], W=[PST[3]], inc=(m == 3))
                        op(DVE, lambda e, o=o: e.tensor_tensor(out=t1[:, 0, :], in0=PS[2][:], in1=ga[:, o, :], op=ALU.mult), R=[PST[2], ga_t], W=[t1_t[0]])
                        op(DVE, lambda e, o=o: e.tensor_tensor(out=t1[:, 1, :], in0=PS[3][:], in1=ga[:, 8 + o, :], op=ALU.mult), R=[PST[3], ga_t], W=[t1_t[1]])
                        op(POOL, lambda e, o=o: e.tensor_tensor(out=mixed[:, o, :], in0=t1[:, 0, :], in1=t1[:, 1, :], op=ALU.add), R=t1_t, W=[mixed_t])
                    for s in range(4):
                        r0 = t * 512 + s * 128
                        dma(SP, xs_[:, s % 2, :], X[r0:r0 + 128, :], W=[xs_t[s % 2]])
                        for hf in range(2):
                            for o in range(8):
                                op(PE, lambda e, o=o, hf=hf, s=s: e.matmul(PS[4 + hf][:], lhsT=mixed[:, o, s * 128:(s + 1) * 128], rhs=wmix[:, o, hf * 512:(hf + 1) * 512], start=(o == 0), stop=(o == 7)),
                                   R=[mixed_t, w_t], W=[PST[4 + hf]], inc=(o == 7))
                        post_norm_res([PS[4][:], PS[5][:]], [PST[4], PST[5]], xs_[:, s % 2, :], xs_t[s % 2], g3[:], g3_t, tmp[:, 0, :], tmp_t,
                                      tmp[:, 0, :], tmp_t, ss2[:, 0, :], ss2_t, rs2[:, 0, :], rs2_t, X[r0:r0 + 128, :])
                fw.barrier()


        def phase_xattn(l):
            fw.barrier()
            with contextlib.ExitStack() as st:
                wq = sb("wq", [128, 8, D], BF16, st)
                wkv = sb("wkv", [128, 8, 2 * D], BF16, st)
                wo = sb("wo", [128, 8, D], BF16, st)
                xt = sb("xt", [128, 4, D], F32, st)
                hb = sb("hb", [128, 2, D], BF16, st)
                hT = sb("hT", [128, 8, 512], BF16, st)
                g4 = sb("g4", [128, D], F32, st)
                g5 = sb("g5", [128, D], F32, st)
                gm = sb("gm", [128, D], F32, st)
                ss = sb("ss", [128, 4], F32, st)
                rs = sb("rs", [128, 4], F32, st)
                kTs = sb("kTs", [128, 2, 8, NMEM], BF16, st)
                vts = sb("vts", [128, 2, 2, D], BF16, st)
                qTs = sb("qTs", [128, 8, 512], BF16, st)
                pex = sb("pex", [128, 4, NMEM], F32, st)
                pnb = sb("pnb", [128, 4, NMEM], BF16, st)
                mst = sb("mst", [128, 3, 4], F32, st)
                pTs = sb("pTs", [128, 8, 512], BF16, st)
                oTs = sb("oTs", [128, 8, 512], BF16, st)
                tmp = sb("tmp", [128, 1, D], F32, st)
                ss2 = sb("ss2", [128, 1, 4], F32, st)
                rs2 = sb("rs2", [128, 1, 1], F32, st)
                w_t = T()
                xt_t = [T() for _ in range(4)]
                hb_t = [T(), T()]
                hT_t, ss_t, rs_t, g4_t, g5_t, gm_t, kTs_t, vts_t, qTs_t, pex_t, pnb_t, mst_t, pTs_t, oTs_t, tmp_t, ss2_t, rs2_t = [T() for _ in range(17)]
                bc_load(g4[:], norm_g[l, 4:5, :], g4_t)
                bc_load(g5[:], norm_g[l, 5:6, :], g5_t)
                bc_load(gm[:], mem_norm_g[l:l + 1, :], gm_t)
                dma(POOL, wq[:], xattn_w_q[l].rearrange("(k p) n -> p k n", p=128), W=[w_t])
                for k in range(8):
                    dma(POOL, wkv[:, k, :], xattn_w_kv[l, k * 128:(k + 1) * 128, :], W=[w_t])
                dma(POOL, wo[:], xattn_w_o[l].rearrange("(k p) n -> p k n", p=128), W=[w_t])
                for ms in range(2):
                    load_norm_T(mem2, 0, xt, xt_t, hb, hb_t, hT, hT_t, gm[:], gm_t, ss, ss_t, rs, rs_t, nsub=2, row0=ms * NMEM)
                    for c in range(8):
                        b = 2 + c % 2
                        for k in range(8):
                            op(PE, lambda e, k=k, c=c, b=b: e.matmul(PS[b][:, 0:NMEM], lhsT=wkv[:, k, c * 128:(c + 1) * 128], rhs=hT[:, k, 0:NMEM], start=(k == 0), stop=(k == 7)),
                               R=[w_t, hT_t], W=[PST[b]], inc=(k == 7))
                        op(ACT, lambda e, c=c, b=b, ms=ms: e.copy(out=kTs[:, ms, c, :], in_=PS[b][:, 0:NMEM]), R=[PST[b]], W=[kTs_t])
                    for mc in range(2):
                        for hf in range(2):
                            b = 4 + hf
                            for k in range(8):
                                op(PE, lambda e, k=k, mc=mc, hf=hf, b=b: e.matmul(PS[b][:], lhsT=hT[:, k, mc * 128:(mc + 1) * 128], rhs=wkv[:, k, D + hf * 512:D + (hf + 1) * 512], start=(k == 0), stop=(k == 7)),
                                   R=[w_t, hT_t], W=[PST[b]], inc=(k == 7))
                            op(DVE, lambda e, mc=mc, hf=hf, b=b, ms=ms: e.tensor_copy(out=vts[:, ms, mc, hf * 512:(hf + 1) * 512], in_=PS[b][:]), R=[PST[b]], W=[vts_t])
                for t in range(NT):
                    ms = 0 if t < NP else 1
                    load_norm_T(X, t, xt, xt_t, hb, hb_t, hT, hT_t, g4[:], g4_t, ss, ss_t, rs, rs_t)
                    for c in range(8):
                        b = 2 + c % 2
                        for k in range(8):
                            op(PE, lambda e, k=k, c=c, b=b: e.matmul(PS[b][:], lhsT=wq[:, k, c * 128:(c + 1) * 128], rhs=hT[:, k, :], start=(k == 0), stop=(k == 7)),
                               R=[w_t, hT_t], W=[PST[b]], inc=(k == 7))
                        op(ACT, lambda e, c=c, b=b: e.activation(out=qTs[:, c, :], in_=PS[b][:], func=AF.Copy, scale=1.0 / 16.0), R=[PST[b]], W=[qTs_t])
                    for s in range(4):
                        for hh in range(4):
                            b = 4 + hh // 2
                            for c in range(2):
                                op(PE, lambda e, hh=hh, c=c, b=b, s=s, ms=ms: e.matmul(PS[b][:, (hh % 2) * NMEM:(hh % 2 + 1) * NMEM], lhsT=qTs[:, 2 * hh + c, s * 128:(s + 1) * 128], rhs=kTs[:, ms, 2 * hh + c, :], start=(c == 0), stop=(c == 1)),
                                   R=[qTs_t, kTs_t], W=[PST[b]], inc=(c == 1 and hh % 2 == 1))
                        for bk in range(2):
                            op(DVE, lambda e, bk=bk: e.tensor_reduce(out=mst[:, 0, 2 * bk:2 * bk + 2], in_=PS[4 + bk][:].rearrange("p (h m) -> p h m", h=2), axis=AX.X, op=ALU.max), R=[PST[4 + bk]], W=[mst_t])
                        op(DVE, lambda e: e.tensor_scalar(out=mst[:, 0, :], in0=mst[:, 0, :], scalar1=-1.0, scalar2=None, op0=ALU.mult), R=[mst_t], W=[mst_t])
                        for hh in range(4):
                            op(ACT, lambda e, hh=hh: e.activation(out=pex[:, hh, :], in_=PS[4 + hh // 2][:, (hh % 2) * NMEM:(hh % 2 + 1) * NMEM], func=AF.Exp, bias=mst[:, 0, hh:hh + 1], accum_out=mst[:, 1, hh:hh + 1]),
                               R=[PST[4 + hh // 2], mst_t], W=[pex_t, mst_t])
                        op(DVE, lambda e: e.reciprocal(out=mst[:, 2, :], in_=mst[:, 1, :]), R=[mst_t], W=[mst_t])
                        op(DVE, lambda e: e.tensor_tensor(out=pnb[:], in0=pex[:], in1=mst[:, 2, :].unsqueeze(2).to_broadcast([128, 4, NMEM]), op=ALU.mult), R=[pex_t, mst_t], W=[pnb_t])
                        for i in range(8):
                            op(PE, lambda e, i=i: e.transpose(out=ps_bf(6)[:, i * 128:(i + 1) * 128], in_=pnb[:, i // 2, (i % 2) * 128:(i % 2 + 1) * 128], identity=ident[:]), R=[pnb_t], W=[PST[6]], inc=(i == 7))
                        op(ACT, lambda e, s=s: e.copy(out=pTs[:, :, s * 128:(s + 1) * 128], in_=ps_bf(6).rearrange("p (k n) -> p k n", k=8)), R=[PST[6]], W=[pTs_t])
                    for c in range(8):
                        hh = c // 2
                        b = 2 + c % 2
                        for mc in range(2):
                            op(PE, lambda e, c=c, hh=hh, mc=mc, b=b, ms=ms: e.matmul(PS[b][:], lhsT=vts[:, ms, mc, c * 128:(c + 1) * 128], rhs=pTs[:, 2 * hh + mc, :], start=(mc == 0), stop=(mc == 1)),
                               R=[vts_t, pTs_t], W=[PST[b]], inc=(mc == 1))
                        op(ACT if c % 2 == 0 else DVE, (lambda e, c=c, b=b: e.copy(out=oTs[:, c, :], in_=PS[b][:])) if c % 2 == 0 else (lambda e, c=c, b=b: e.tensor_copy(out=oTs[:, c, :], in_=PS[b][:])), R=[PST[b]], W=[oTs_t])
                    for s in range(4):
                        r0 = t * 512 + s * 128
                        for hf in range(2):
                            for c in range(8):
                                op(PE, lambda e, c=c, hf=hf, s=s: e.matmul(PS[4 + hf][:], lhsT=oTs[:, c, s * 128:(s + 1) * 128], rhs=wo[:, c, hf * 512:(hf + 1) * 512], start=(c == 0), stop=(c == 7)),
                                   R=[oTs_t, w_t], W=[PST[4 + hf]], inc=(c == 7))
                        post_norm_res([PS[4][:], PS[5][:]], [PST[4], PST[5]], xt[:, s, :], xt_t[s], g5[:], g5_t, tmp[:, 0, :], tmp_t,
                                      tmp[:, 0, :], tmp_t, ss2[:, 0, :], ss2_t, rs2[:, 0, :], rs2_t, X[r0:r0 + 128, :])
                fw.barrier()

        cur = xin
        stage = 0
        for l in range(2):
            last = (l == 1)
            if fast:
                ret_tables(l)
                phase_mixer(l, SBD)
                break
            phase_ffn(l, 0, cur, X)
            cur = X
            stage += 1
            if upto <= stage:
                break
            phase_inproj(l)
            if dbg == "inproj":
                break
            phase_scan(l, SBD)
            if dbg == "scan":
                break
            phase_mixer(l, SBD)
            stage += 1
            if upto <= stage:
                break
            phase_xattn(l)
            stage += 1
            if upto <= stage:
                break
            phase_ffn(l, 1, cur, yout if last else X)
            stage += 1
            if upto <= stage:
                break
        if upto <= 7:
            fw.barrier()
            for t in range(NT):
                dma(SP, yout[t * 512:(t + 1) * 512, :], X[t * 512:(t + 1) * 512, :])
        fw.barrier()
    return nc


def _consts():
    c = np.zeros((128, 388), np.float32)
    i = np.arange(128, dtype=np.float32)
    c[:, 0] = i + 1
    c[:, 1] = 128 - i
    c[:, 2] = 127 - i
    c[:, 3] = i
    rel = i[None, :] - i[:, None]
    c[:, 4:132] = np.maximum(rel, 0)
    c[:, 132:260] = np.maximum(-rel, 0)
    c[:, 260:388] = np.eye(128, dtype=np.float32)
    return c


def _rot_table(pos):
    half = 32
    inv_freq = (10000.0 ** (-np.arange(half, dtype=np.float32) / half)).astype(np.float32)
    ang = pos.astype(np.float32)[:, None] * inv_freq[None, :]
    return np.concatenate([np.cos(ang), np.cos(ang), -np.sin(ang), np.sin(ang)], axis=-1).astype(np.float32)


def _sel(c, nchs):
    s = np.zeros((128, 48), np.float32)
    for cp in range(NCORE):
        s[:, cp] = 1.0 if cp == c - 1 else 0.0
        s[:, 8 + cp] = 1.0 if cp == c + 1 else 0.0
        s[:64, 16 + cp] = 1.0 if cp < c else 0.0
        s[:64, 24 + cp] = max(c - 1 - cp, 0) * 128.0 * nchs
        s[64:, 16 + cp] = 1.0 if cp > c else 0.0
        s[64:, 24 + cp] = max(cp - c - 1, 0) * 128.0 * nchs
    return s


def run(inputs, NP, NS, upto=99, dbg=False):
    nc = build(NP, NS, upto=upto, dbg=dbg)
    xp = np.asarray(inputs["x_prompt"], np.float32)
    xs = np.asarray(inputs["x_sample"], np.float32)
    mp = np.asarray(inputs["mem_prompt"], np.float32)
    ms = np.asarray(inputs["mem_sample"], np.float32)
    TP, TS = 512 * NP, 512 * NS
    assert xp.shape[1] == TP and xs.shape[1] == TS * NCORE
    wnames = ["norm_g", "ffn1_w_gu", "ffn1_w_down", "w_in", "conv_dw_w", "conv_dw_b", "conv_ln_g", "conv_ln_b", "conv_w_pw",
              "ret_decay_fwd", "ret_decay_bwd", "ret_gn_g", "ret_w_out", "gate_b", "w_mix_out", "mem_norm_g",
              "xattn_w_q", "xattn_w_kv", "xattn_w_o", "ffn2_w_gu", "ffn2_w_down"]
    shared = {n: np.ascontiguousarray(np.asarray(inputs[n], np.float32)) for n in wnames}
    cst = _consts()
    in_maps = []
    for c in range(NCORE):
        m = dict(shared)
        m["xin"] = np.ascontiguousarray(np.concatenate([xp[c], xs[0, c * TS:(c + 1) * TS]], axis=0))
        m["mem2"] = np.ascontiguousarray(np.concatenate([mp[c], ms[0]], axis=0))
        pos = np.concatenate([np.arange(TP), c * TS + np.arange(TS)])
        m["rot"] = _rot_table(pos)
        m["cst"] = cst
        m["sel"] = _sel(c, 4 * NS)
        in_maps.append(m)
    res = run_bass_kernel_spmd(nc, in_maps, core_ids=list(range(NCORE)))
    yp = np.stack([res.results[c]["yout"][:TP] for c in range(NCORE)], axis=0)
    ys = np.concatenate([res.results[c]["yout"][TP:] for c in range(NCORE)], axis=0)[None]
    return yp.astype(np.float32), ys.astype(np.float32)


def kernel(**inputs):
    return run(inputs, 8, 4)
```
